# Optimizing a Trainium2 kernel written in Bass

```python
import jax, jax.numpy as jnp
from jax import lax
import numpy as np

D_MODEL = 1024
BATCH = 16
SEQ = 2048
DEPTH = 2

N_MIXERS = 2
N_A_LAYERS = (DEPTH + 1) // 2
N_B_LAYERS = DEPTH // 2
D_FF = ((8 * D_MODEL // 3 + 127) // 128) * 128
CHUNK = 128
A_DIM = 2 * D_MODEL
A_HEADS = 8
A_HEAD_DIM = A_DIM // A_HEADS
POOL_WINDOWS = (2, 4, 8, 16)
B_GROUPS = len(POOL_WINDOWS)
B_DIM = D_MODEL
B_GROUP_DIM = B_DIM // B_GROUPS
EPS = 1e-6

kernel_name = "hybrid_sgu_pool_macaron"


def rmsnorm(x, g):
    xf = x.astype(jnp.float32)
    y = xf * lax.rsqrt(jnp.mean(xf * xf, axis=-1, keepdims=True) + EPS)
    return (y * g.astype(jnp.float32)).astype(x.dtype)


def swiglu(h, w_in, w_out):
    gate, up = jnp.split(h @ w_in, 2, axis=-1)
    return (jax.nn.silu(gate) * up) @ w_out


def chunked_sgu(h, w_in, v_norm, w_s, b_s, w_out):
    bsz, seq, _ = h.shape
    z = jax.nn.gelu(h @ w_in)
    u, v = jnp.split(z, 2, axis=-1)
    v = rmsnorm(v, v_norm)
    v = v.reshape(bsz, seq // CHUNK, CHUNK, A_HEADS, A_HEAD_DIM)
    causal = jnp.tril(jnp.ones((CHUNK, CHUNK), dtype=bool))
    ws = jnp.where(causal[None], w_s, jnp.zeros_like(w_s))
    gate = jnp.einsum('hts,bcshd->bcthd', ws, v) + b_s.T[None, None, :, :, None]
    gate = gate.reshape(bsz, seq, A_DIM)
    return (u * gate) @ w_out


def causal_mean(x, window):
    seq = x.shape[1]
    c = jnp.cumsum(x.astype(jnp.float32), axis=1)
    c_prev = jnp.pad(c, ((0, 0), (window, 0), (0, 0)))[:, :seq]
    cnt = jnp.minimum(jnp.arange(1, seq + 1), window).astype(jnp.float32)[None, :, None]
    return ((c - c_prev) / cnt).astype(x.dtype)


def pool_mixer(h, w_in, w_grp, scale, w_out):
    bsz, seq, _ = h.shape
    p = h @ w_in
    groups = jnp.split(p, B_GROUPS, axis=-1)
    pooled = jnp.stack([causal_mean(g, w) - g for g, w in zip(groups, POOL_WINDOWS)], axis=2)
    y = jnp.einsum('bsgc,gcd->bsgd', pooled, w_grp).reshape(bsz, seq, B_DIM)
    return (y * scale) @ w_out


def setup_inputs(seed: int = 0) -> dict:
    key = jax.random.key(seed)
    ks = jax.random.split(key, 16)
    f32 = jnp.float32
    nrm = lambda k, shape, fan_in: jax.random.normal(k, shape, f32) * (fan_in ** -0.5)
    gain = lambda k, shape: 1.0 + 0.02 * jax.random.normal(k, shape, f32)
    return {
        "x": jax.random.normal(ks[0], (BATCH, SEQ, D_MODEL), f32),
        "ffn_norm": gain(ks[1], (DEPTH, 2, D_MODEL)),
        "ffn_w_in": nrm(ks[2], (DEPTH, 2, D_MODEL, 2 * D_FF), D_MODEL),
        "ffn_w_out": nrm(ks[3], (DEPTH, 2, D_FF, D_MODEL), D_FF),
        "mix_norm": gain(ks[4], (DEPTH, D_MODEL)),
        "a_w_in": nrm(ks[5], (N_A_LAYERS, D_MODEL, 2 * A_DIM), D_MODEL),
        "a_v_norm": gain(ks[6], (N_A_LAYERS, A_DIM)),
        "a_w_s": nrm(ks[7], (N_A_LAYERS, A_HEADS, CHUNK, CHUNK), CHUNK),
        "a_b_s": 1.0 + 0.02 * jax.random.normal(ks[8], (N_A_LAYERS, A_HEADS, CHUNK), f32),
        "a_w_out": nrm(ks[9], (N_A_LAYERS, A_DIM, D_MODEL), A_DIM),
        "b_w_in": nrm(ks[10], (N_B_LAYERS, D_MODEL, B_DIM), D_MODEL),
        "b_w_grp": nrm(ks[11], (N_B_LAYERS, B_GROUPS, B_GROUP_DIM, B_GROUP_DIM), B_GROUP_DIM),
        "b_scale": gain(ks[12], (N_B_LAYERS, B_DIM)),
        "b_w_out": nrm(ks[13], (N_B_LAYERS, B_DIM, D_MODEL), B_DIM),
        "final_norm": gain(ks[14], (D_MODEL,)),
    }


def reference(x, ffn_norm, ffn_w_in, ffn_w_out, mix_norm, a_w_in, a_v_norm, a_w_s, a_b_s,
              a_w_out, b_w_in, b_w_grp, b_scale, b_w_out, final_norm):
    for i in range(DEPTH):
        x = x + 0.5 * swiglu(rmsnorm(x, ffn_norm[i, 0]), ffn_w_in[i, 0], ffn_w_out[i, 0])
        h = rmsnorm(x, mix_norm[i])
        j = i // N_MIXERS
        if i % N_MIXERS == 0:
            x = x + chunked_sgu(h, a_w_in[j], a_v_norm[j], a_w_s[j], a_b_s[j], a_w_out[j])
        else:
            x = x + pool_mixer(h, b_w_in[j], b_w_grp[j], b_scale[j], b_w_out[j])
        x = x + 0.5 * swiglu(rmsnorm(x, ffn_norm[i, 1]), ffn_w_in[i, 1], ffn_w_out[i, 1])
    return rmsnorm(x, final_norm)
```

```python
import collections
import contextlib
import numpy as np
import concourse.bass as bass
import concourse.mybir as mybir
from concourse.bass_utils import run_bass_kernel_spmd

F32 = mybir.dt.float32
BF16 = mybir.dt.bfloat16
AF = mybir.ActivationFunctionType
ALU = mybir.AluOpType
AX = mybir.AxisListType

NCORE = 8
D = 1024
DFF = 2816
STK = 2048
GK = 512
NG = 4
EPS = 1e-6
POOLW = (2, 4, 8, 16)
FFN_PARTS = [(0, 4), (4, 4), (8, 3), (11, 3), (14, 4), (18, 4)]
SLOT = 12288
ARENA_F32 = 12288

VC_FFN = 0
VC_MIX = 32
VC_FIN = 48
VC_BSC = 56
VC_VN = 64
NVEC = 80


class Stream:
    def __init__(self, name, step):
        self.name = name
        self.step = step
        self.count = 0
        self.sem = None


class Res:
    def __init__(self, name, readers=None):
        self.name = name
        self.writers = {}
        self.readers = dict(readers) if readers else {}


class Sched:
    ENGS = ("pe", "act", "dve", "pool", "sp")

    def __init__(self):
        self.ops = {e: [] for e in self.ENGS}
        self.waited = {e: {} for e in self.ENGS}
        self.streams = []
        self.cstream = {e: self.new_stream(e, 1) for e in ("pe", "act", "dve", "pool")}
        self.phase_res = []
        self.prev_users = {}

    def new_stream(self, name, step):
        s = Stream(name, step)
        self.streams.append(s)
        return s

    def res(self, name):
        r = Res(name, self.prev_users)
        self.phase_res.append(r)
        return r

    def barrier(self):
        pu = dict(self.prev_users)
        for r in self.phase_res:
            for d in (r.readers, r.writers):
                for s, c in d.items():
                    if pu.get(s, 0) < c:
                        pu[s] = c
        self.prev_users = pu
        self.phase_res = []

    def emit(self, eng, fn, reads=(), writes=(), stream=None, nsig=1):
        own = self.cstream.get(eng)
        need = {}

        def add(d, skip_own):
            for s, c in d.items():
                if skip_own and s is own:
                    continue
                if need.get(s, 0) < c:
                    need[s] = c

        for r in reads:
            add(r.writers, False)
        for w in writes:
            add(w.readers, False)
            add(w.writers, False)
        wd = self.waited[eng]
        waits = []
        for s, c in need.items():
            if wd.get(s, 0) < c:
                wd[s] = c
                waits.append((s, c))
        st = stream if stream is not None else own
        st.count += nsig
        tok = st.count
        self.ops[eng].append((waits, fn, st, nsig))
        for r in reads:
            if r.readers.get(st, 0) < tok:
                r.readers[st] = tok
        for w in writes:
            w.writers = {st: tok}
            w.readers = {}

    def run_engine(self, eng, e):
        for waits, fn, st, nsig in self.ops[eng]:
            for s, c in waits:
                e.wait_ge(s.sem, c * s.step)
            inss = fn(e)
            assert len(inss) == nsig
            for ins in inss:
                ins.then_inc(st.sem, st.step)


def mm_group(e, out, pairs):
    n = len(pairs)
    ins = None
    for i, (l, r) in enumerate(pairs):
        ins = e.matmul(out, l, r, start=(i == 0), stop=(i == n - 1))
    return [ins]


def build_program(n_st=2, sublayers=("f00", "mA", "f01", "f10", "mB", "f11"), final=True):
    nc = bass.Bass("TRN2", target_bir_lowering=False)
    ntok = n_st * STK

    def din(name, shape):
        return nc.dram_tensor(name, list(shape), F32, kind="ExternalInput").ap()

    xT = din("xT", [D, ntok])
    outT = nc.dram_tensor("outT", [D, ntok], F32, kind="ExternalOutput").ap()
    ffn_w_in = din("ffn_w_in", [4, D, 2 * DFF])
    ffn_w_out = din("ffn_w_out", [4, DFF, D])
    a_w_in = din("a_w_in", [D, 4096])
    a_w_out = din("a_w_out", [2048, D])
    b_w_in = din("b_w_in", [D, D])
    b_w_grp = din("b_w_grp", [1024, 256])
    b_w_out = din("b_w_out", [D, D])
    vec_d = din("vec", [128, NVEC])
    wsT_d = din("wsT", [128, 1024])
    mask_d = din("mask", [128, 128])
    bias_d = din("bias_bc", [128, 1024])
    invc_d = din("invc", [128, 64])

    xT3 = xT.rearrange("(c p) t -> p c t", p=128)
    outT3 = outT.rearrange("(c p) t -> p c t", p=128)

    S = Sched()
    es = contextlib.ExitStack()
    with es:
        def sb(name, shape, dt):
            return es.enter_context(nc.sbuf_tensor(name, list(shape), dt))

        xs = sb("xs", [128, 8, STK], F32)
        hT = sb("hT", [128, 8, STK], BF16)
        ring = sb("ring", [128, 2, SLOT], BF16)
        arena = sb("arena", [128, ARENA_F32], F32)
        vec = sb("vec_t", [128, NVEC], F32)
        ones = sb("ones", [128, 128], BF16)
        wsT = sb("wsT_t", [128, 8, 128], BF16)
        bias_bc = sb("bias_t", [128, 8, 128], F32)
        invc = sb("invc_t", [128, 4, 16], F32)
        epst = sb("eps_t", [128, 1], F32)
        ss = sb("ss_t", [128, 32], F32)
        ssum = sb("ssum_t", [128, 8], F32)
        rt8 = sb("rt8_t", [128, 8], F32)
        rstdv = sb("rstdv_t", [128, 8], F32)
        sq = sb("sq_t", [128, 4, 512], BF16)
        rstd_bufs = [sb("rstd_t", [128, 512], F32), sb("rstd2_t", [128, 512], F32)]
        banks = [es.enter_context(nc.psum_tensor(f"bank{i}", [128, 512], F32)) for i in range(8)]

        for s in S.streams:
            s.sem = es.enter_context(nc.semaphore(f"sem_{s.name}"))

        def dstream(name):
            s = S.new_stream(name, 16)
            s.sem = es.enter_context(nc.semaphore(f"sem_{name}"))
            return s

        ring_st = [dstream("ring0"), dstream("ring1")]
        cst_st = dstream("cst")
        ldx_st = [dstream(f"ldx{g}") for g in range(NG)]
        ldxb_st = [dstream(f"ldxb{g}") for g in range(NG)]
        stx_st = [dstream(f"stx{g}") for g in range(NG)]

        xs_r = [[Res(f"xs{c}_{g}") for g in range(NG)] for c in range(8)]
        hT_r = [Res(f"hT{g}") for g in range(NG)]
        ring_r = [Res("ring0"), Res("ring1")]
        bank_r = [Res(f"bank{i}") for i in range(8)]
        const_r = Res("const")
        ones_r = Res("ones")
        wsT_r = Res("wsT")
        small_r = Res("small")
        sq_r = Res("sq")
        rstd_rs = [Res("rstd"), Res("rstd2")]
        ncount = [0]

        def carve(off_bytes, shape, dt):
            n = int(np.prod(shape))
            o = off_bytes // 4
            if dt == F32:
                a = arena[:, o:o + n]
            else:
                a = arena[:, o:o + n // 2].bitcast(BF16)
            if len(shape) == 2:
                a = a.rearrange("p (a b) -> p a b", a=shape[0])
            return a

        KB = 1024

        def dma_op(eng, pairs, stream, reads=(), writes=()):
            def fn(e, pairs=pairs):
                return [e.dma_start(out=o, in_=i) for (o, i) in pairs]
            S.emit(eng, fn, reads=reads, writes=writes, stream=stream, nsig=len(pairs))

        wsraw = carve(0, (8, 128), F32)
        maskt = carve(4 * KB, (128,), F32)
        setup_r = S.res("setup")
        dma_op("pool", [(vec[:], vec_d[:, :]),
                      (bias_bc[:].rearrange("p h t -> p (h t)"), bias_d[:, :]),
                      (invc[:].rearrange("p a b -> p (a b)"), invc_d[:, :]),
                      (wsraw.rearrange("p h t -> p (h t)"), wsT_d[:, :]),
                      (maskt, mask_d[:, :])],
               cst_st, writes=[const_r, setup_r])
        S.emit("dve", lambda e: [e.memset(ones[:], 1.0)], writes=[ones_r])
        S.emit("dve", lambda e: [e.memset(epst[:], EPS)], writes=[const_r])
        S.emit("dve", lambda e: [e.tensor_tensor(
            out=wsT[:], in0=wsraw, in1=maskt.unsqueeze(1).broadcast_to([128, 8, 128]), op=ALU.mult)],
            reads=[setup_r, const_r], writes=[wsT_r])
        S.barrier()

        def wv(ap2d, c0, c1):
            return ap2d.rearrange("(k p) c -> p k c", p=128)[:, :, c0:c1]

        def rows(ap2d, r0, nchunk):
            return ap2d[r0 * 128:(r0 + nchunk) * 128, :].rearrange("(j p) c -> p j c", p=128)

        plan = []

        def plan_ffn(widx):
            idx = []
            for (j0, n) in FFN_PARTS:
                w_in = ffn_w_in[widx]
                w_out = ffn_w_out[widx]
                plan.append([(0, 8, n * 128, wv(w_in, j0 * 128, (j0 + n) * 128)),
                             (4096, 8, n * 128, wv(w_in, DFF + j0 * 128, DFF + (j0 + n) * 128)),
                             (8192, n, 1024, rows(w_out, j0, n))])
                idx.append(len(plan) - 1)
            return idx

        def plan_mixA():
            idx = []
            for blk in range(2):
                for vp in range(2):
                    plan.append([(0, 8, 1024, wv(a_w_in, 2048 + vp * 1024, 2048 + (vp + 1) * 1024))])
                    idx.append(len(plan) - 1)
                for hp in range(4):
                    plan.append([(0, 8, 512, wv(a_w_in, hp * 512, (hp + 1) * 512)),
                                 (4096, 4, 1024, rows(a_w_out, hp * 4, 4))])
                    idx.append(len(plan) - 1)
            return idx

        def plan_mixB():
            plan.append([(0, 8, 1024, wv(b_w_in, 0, 1024)),
                         (8192, 8, 256, rows(b_w_grp, 0, 8))])
            plan.append([(0, 8, 1024, wv(b_w_out, 0, 1024))])
            return [len(plan) - 2, len(plan) - 1]

        seq = []
        for st_i in range(n_st):
            for sl in sublayers:
                if sl[0] == "f":
                    i, k = int(sl[1]), int(sl[2])
                    seq.append((st_i, sl, plan_ffn(i * 2 + k)))
                elif sl == "mA":
                    seq.append((st_i, sl, plan_mixA()))
                else:
                    seq.append((st_i, sl, plan_mixB()))

        loaded = [0]

        def slot_view(p, j):
            off, k, c, _ = plan[p][j]
            return ring[:, p % 2, off:off + k * c].rearrange("p (k c) -> p k c", k=k)

        def ensure(p):
            while loaded[0] <= min(p, len(plan) - 1):
                q = loaded[0]
                pairs = [(slot_view(q, j), plan[q][j][3]) for j in range(len(plan[q]))]
                dma_op("pool", pairs, ring_st[q % 2], writes=[ring_r[q % 2]])
                loaded[0] += 1
            return p % 2

        def done(p):
            ensure(p + 2)

        bank_rot = [0]

        def nb(lst):
            b = lst[bank_rot[0] % len(lst)]
            bank_rot[0] += 1
            return b

        def gsl(g):
            return slice(g * GK, (g + 1) * GK)

        def PE(out, pairs, reads, writes):
            S.emit("pe", lambda e: mm_group(e, out, pairs), reads=reads, writes=writes)

        def ACTV(out, in_, func, reads, writes, **kw):
            S.emit("act", lambda e: [e.activation(out=out, in_=in_, func=func, **kw)], reads=reads, writes=writes)

        def TT(eng, out, in0, in1, op, reads, writes):
            S.emit(eng, lambda e: [e.tensor_tensor(out=out, in0=in0, in1=in1, op=op)], reads=reads, writes=writes)

        def STT(out, in0, scalar, in1, op0, op1, reads, writes):
            S.emit("dve", lambda e: [e.scalar_tensor_tensor(out=out, in0=in0, scalar=scalar, in1=in1,
                                                            op0=op0, op1=op1)], reads=reads, writes=writes)

        def norm_gen(g, gcol, dst, dst_r):
            t = gsl(g)
            rstd = rstd_bufs[ncount[0] % 2]
            rstd_r = rstd_rs[ncount[0] % 2]
            ncount[0] += 1
            for half in range(2):
                def sqfn(e, half=half):
                    ins = None
                    for cc in range(4):
                        ins = e.activation(out=sq[:, cc, :], in_=xs[:, half * 4 + cc, t], func=AF.Square)
                    return [ins]
                S.emit("act", sqfn, reads=[xs_r[half * 4 + cc][g] for cc in range(4)], writes=[sq_r])
                yield

                def stfn(e, half=half):
                    ins = None
                    for cc in range(4):
                        ins = e.matmul(banks[6][:], ones[:], sq[:, cc, :],
                                       start=(half == 0 and cc == 0), stop=(half == 1 and cc == 3))
                    return [ins]
                S.emit("pe", stfn, reads=[sq_r, ones_r], writes=[bank_r[6]])
                yield ("heavy" if half == 1 else None)
            ACTV(rstd[:], banks[6][:], AF.Ln, [bank_r[6], const_r], [rstd_r], bias=epst[:, 0:1], scale=1.0 / D)
            ACTV(rstd[:], rstd[:], AF.Exp, [rstd_r], [rstd_r], scale=-0.5)
            yield
            for c in range(8):
                if c == 4:
                    yield
                STT(dst[:, c, t], xs[:, c, t], vec[:, gcol + c:gcol + c + 1], rstd[:], ALU.mult, ALU.mult,
                    [xs_r[c][g], rstd_r, const_r], [dst_r(c, g)])

        finq = collections.deque()
        normq = collections.deque()
        hook_done = set()
        cur_hook = [None]

        def pop_hook(kind="any"):
            if cur_hook[0] is None:
                q = finq if finq else normq
                if not q:
                    return
                key, gen = q.popleft()
                cur_hook[0] = [key, gen, None]
            key, gen, tag = cur_hook[0]
            if tag == "heavy" and kind == "light":
                return
            try:
                cur_hook[0][2] = next(gen)
            except StopIteration:
                hook_done.add(key)
                cur_hook[0] = None

        def flush_fins():
            while finq or (cur_hook[0] is not None and cur_hook[0][0][0] == "fin"):
                pop_hook()

        def require(key):
            while key not in hook_done:
                assert cur_hook[0] is not None or finq or normq, f"missing hook {key}"
                pop_hook()

        def gcol_of(sl):
            if sl[0] == "f":
                return VC_FFN + (int(sl[1]) * 2 + int(sl[2])) * 8
            return VC_MIX + (0 if sl == "mA" else 8)

        def load_group(st_i, g):
            tsl = slice(st_i * STK + g * GK, st_i * STK + (g + 1) * GK)
            dma_op("sp", [(xs[:, 0:4, gsl(g)], xT3[:, 0:4, tsl])], ldx_st[g],
                   writes=[xs_r[c][g] for c in range(4)])
            dma_op("sp", [(xs[:, 4:8, gsl(g)], xT3[:, 4:8, tsl])], ldxb_st[g],
                   writes=[xs_r[c][g] for c in range(4, 8)])

        def push_post(n, g):
            st_i = seq[n][0]
            last_of_st = (n + 1 == len(seq)) or (seq[n + 1][0] != st_i)
            if last_of_st:
                def fin(st_i=st_i, g=g):
                    if final:
                        yield from norm_gen(g, VC_FIN, xs, lambda c, g: xs_r[c][g])
                    tsl = slice(st_i * STK + g * GK, st_i * STK + (g + 1) * GK)
                    dma_op("sp", [(outT3[:, :, tsl], xs[:, :, gsl(g)])], stx_st[g],
                           reads=[xs_r[c][g] for c in range(8)])
                    if n + 1 < len(seq):
                        load_group(seq[n + 1][0], g)
                finq.append((("fin", n, g), fin()))
            if n + 1 < len(seq):
                normq.append((("norm", n + 1, g), norm_gen(g, gcol_of(seq[n + 1][1]), hT, lambda c, g: hT_r[g])))

        def emit_ffn(n_inst, widx, pieces):
            actT = [carve(0, (4, 512), BF16), carve(4 * KB, (4, 512), BF16)]
            sg = [carve(8 * KB, (512,), F32), carve(10 * KB, (512,), F32)]
            actT_r = [S.res("actT0"), S.res("actT1")]
            sg_r = [S.res("sg0"), S.res("sg1")]
            gcol = VC_FFN + widx * 8
            pairs = [(q, g) for q in range(len(FFN_PARTS)) for g in range(NG)]
            cnt = [0]
            ocnt = [0]
            nxt_mixer = (n_inst + 1 < len(seq) and seq[n_inst + 1][0] == seq[n_inst][0]
                         and seq[n_inst + 1][1][0] == "m")

            def gu_kind(q):
                if q == 0:
                    return "any"
                return "light"

            def GU(i):
                q, g = pairs[i]
                j0, n = FFN_PARTS[q]
                if g == 0:
                    ensure(pieces[q])
                if q == 0:
                    require(("norm", n_inst, g))
                p = pieces[q]
                s = p % 2
                wg, wu = slot_view(p, 0), slot_view(p, 1)
                t = gsl(g)
                a = i % 2
                for jj in range(n):
                    m = cnt[0]
                    cnt[0] += 1
                    bG, bU = m % 2, 2 + m % 2
                    PE(banks[bG][:], [(wg[:, kk, jj * 128:(jj + 1) * 128], hT[:, kk, t]) for kk in range(8)],
                       [ring_r[s], hT_r[g]], [bank_r[bG]])
                    pop_hook(gu_kind(q))
                    PE(banks[bU][:], [(wu[:, kk, jj * 128:(jj + 1) * 128], hT[:, kk, t]) for kk in range(8)],
                       [ring_r[s], hT_r[g]], [bank_r[bU]])
                    ACTV(sg[m % 2], banks[bG][:], AF.Silu, [bank_r[bG]], [sg_r[m % 2]])
                    TT("dve", actT[a][:, jj, :], sg[m % 2], banks[bU][:], ALU.mult,
                       [sg_r[m % 2], bank_r[bU]], [actT_r[a]])
                    pop_hook(gu_kind(q))
                    if q == 0 or finq:
                        pop_hook(gu_kind(q))

            def WO(i):
                q, g = pairs[i]
                j0, n = FFN_PARTS[q]
                p = pieces[q]
                s = p % 2
                wo = slot_view(p, 2)
                t = gsl(g)
                a = i % 2
                pop_hook("any")
                for dc in range(8):
                    bO = (4, 5, 7)[ocnt[0] % 3]
                    ocnt[0] += 1
                    PE(banks[bO][:], [(wo[:, jj, dc * 128:(dc + 1) * 128], actT[a][:, jj, :]) for jj in range(n)],
                       [ring_r[s], actT_r[a]], [bank_r[bO]])
                    STT(xs[:, dc, t], banks[bO][:], 0.5, xs[:, dc, t], ALU.mult, ALU.add,
                        [bank_r[bO], xs_r[dc][g]], [xs_r[dc][g]])
                if g == NG - 1:
                    done(p)
                if q == len(FFN_PARTS) - 1:
                    push_post(n_inst, g)

            GU(0)
            for i in range(1, len(pairs)):
                GU(i)
                WO(i - 1)
            WO(len(pairs) - 1)
            flush_fins()
            S.barrier()

        def emit_mixA(n_inst, pieces):
            for g in range(NG):
                require(("norm", n_inst, g))
            v = carve(0, (8, 2048), BF16)
            uT = carve(32 * KB, (2, 512), F32)
            ugT = [carve(36 * KB, (4, 512), BF16), carve(40 * KB, (4, 512), BF16)]
            tmp = [carve(44 * KB, (512,), F32), carve(46 * KB, (512,), F32)]
            junk = ugT[0][:, 0, :]
            pi = 0
            ucnt = 0
            for blk in range(2):
                require(("norm", n_inst, blk * 2))
                require(("norm", n_inst, blk * 2 + 1))
                v_r = [S.res(f"v{tt}") for tt in range(8)]
                uT_r = [S.res("uT0"), S.res("uT1")]
                ugT_r = [S.res("ugT0"), S.res("ugT1")]
                tmp_r = [S.res("tmp0"), S.res("tmp1")]
                S.emit("dve", lambda e: [e.memset(ss[:], 0.0)], reads=[small_r], writes=[small_r])
                pend = None
                for vp in range(2):
                    p = pieces[pi]
                    pi += 1
                    s = ensure(p)
                    wvv = slot_view(p, 0)
                    for vb in range(2):
                        col = (vp * 2 + vb) * 512
                        for tt in range(8):
                            pop_hook()
                            g = blk * 2 + tt // 4
                            tok = slice((blk * 8 + tt) * 128, (blk * 8 + tt + 1) * 128)
                            b = nb([0, 1, 2, 3])
                            PE(banks[b][:], [(hT[:, kk, tok], wvv[:, kk, vb * 512:(vb + 1) * 512]) for kk in range(8)],
                               [ring_r[s], hT_r[g]], [bank_r[b]])
                            ACTV(v[:, tt, col:col + 512], banks[b][:], AF.Gelu_apprx_tanh, [bank_r[b]], [v_r[tt]])
                            if pend is not None:
                                ACTV(*pend[0], **pend[1])
                            sidx = tt * 4 + vp * 2 + vb
                            pend = ((junk, v[:, tt, col:col + 512], AF.Square, [v_r[tt]], [ugT_r[0], small_r]),
                                    dict(accum_out=ss[:, sidx:sidx + 1]))
                    done(p)
                ACTV(*pend[0], **pend[1])
                S.emit("dve", lambda e: [e.tensor_reduce(
                    out=ssum[:], in_=ss[:].rearrange("p (t f) -> p t f", f=4), axis=AX.X, op=ALU.add)],
                    reads=[small_r], writes=[small_r])
                ACTV(rt8[:], ssum[:], AF.Sqrt, [small_r, const_r], [small_r], bias=epst[:, 0:1], scale=1.0 / 2048)
                S.emit("dve", lambda e: [e.reciprocal(out=rstdv[:], in_=rt8[:])], reads=[small_r], writes=[small_r])
                pendWO = [None]

                def run_pending():
                    if pendWO[0] is not None:
                        pendWO[0]()
                        pendWO[0] = None

                for hp in range(4):
                    p = pieces[pi]
                    pi += 1
                    s = ensure(p)
                    wu, wo = slot_view(p, 0), slot_view(p, 1)
                    for gi in range(2):
                        g = blk * 2 + gi
                        t = gsl(g)
                        a = ucnt % 2
                        ucnt += 1
                        for tl in range(4):
                            tt = gi * 4 + tl

                            def vn(e, tt=tt, hp=hp):
                                return [e.tensor_scalar_mul(out=v[:, tt, hp * 512:(hp + 1) * 512],
                                                            in0=v[:, tt, hp * 512:(hp + 1) * 512],
                                                            scalar1=rstdv[:, tt:tt + 1])]
                            S.emit("dve", vn, reads=[v_r[tt], small_r], writes=[v_r[tt]])
                        for hh in range(2):
                            h = hp * 2 + hh
                            for uc in range(2):
                                b = nb([0, 1])
                                cu = (hh * 2 + uc) * 128
                                PE(banks[b][:], [(wu[:, kk, cu:cu + 128], hT[:, kk, t]) for kk in range(8)],
                                   [ring_r[s], hT_r[g]], [bank_r[b]])
                                ACTV(uT[:, uc, :], banks[b][:], AF.Gelu_apprx_tanh, [bank_r[b]], [uT_r[uc]])
                                pop_hook()
                            for dc in range(2):
                                b = nb([2, 3])
                                gp = [(banks[b][:, tl * 128:(tl + 1) * 128],
                                       v[:, gi * 4 + tl, h * 256 + dc * 128:h * 256 + (dc + 1) * 128],
                                       wsT[:, h, :]) for tl in range(4)]

                                def gate_mm(e, gp=gp):
                                    ins = None
                                    for (o, l, r) in gp:
                                        ins = e.matmul(o, l, r, start=True, stop=True)
                                    return [ins]
                                S.emit("pe", gate_mm, reads=[v_r[gi * 4 + tl] for tl in range(4)] + [wsT_r],
                                       writes=[bank_r[b]])
                                vc = VC_VN + h * 2 + dc
                                STT(tmp[dc].rearrange("p (a t) -> p a t", a=4),
                                    banks[b][:].rearrange("p (a t) -> p a t", a=4),
                                    vec[:, vc:vc + 1],
                                    bias_bc[:, h, :].unsqueeze(1).broadcast_to([128, 4, 128]),
                                    ALU.mult, ALU.add, [bank_r[b], const_r], [tmp_r[dc]])
                                TT("pool", ugT[a][:, hh * 2 + dc, :], tmp[dc], uT[:, dc, :], ALU.mult,
                                   [tmp_r[dc], uT_r[dc]], [ugT_r[a]])
                                pop_hook()
                        run_pending()

                        def wo_fn(p=p, s=s, wo=wo, g=g, t=t, a=a, hp=hp, gi=gi):
                            for oc in range(8):
                                b = nb([4, 5, 7])
                                PE(banks[b][:], [(wo[:, cc, oc * 128:(oc + 1) * 128], ugT[a][:, cc, :]) for cc in range(4)],
                                   [ring_r[s], ugT_r[a]], [bank_r[b]])
                                TT("dve", xs[:, oc, t], banks[b][:], xs[:, oc, t], ALU.add,
                                   [bank_r[b], xs_r[oc][g]], [xs_r[oc][g]])
                            if hp == 3:
                                push_post(n_inst, g)
                            if gi == 1:
                                done(p)
                        pendWO[0] = wo_fn
                run_pending()
                S.barrier()

        def emit_mixB(n_inst, pieces):
            for g in range(NG):
                require(("norm", n_inst, g))
            pT = carve(0, (8, 528), F32)
            ping = [carve(16896, (2, 528), F32), carve(16896 + 4224, (2, 528), F32)]
            pooled = carve(25344, (8, 512), BF16)
            yT = carve(33536, (8, 512), BF16)
            fix = carve(41728, (2, 16), F32)
            pT_r = [S.res(f"pT{wi}") for wi in range(4)]
            ping_r = [S.res("ping0"), S.res("ping1")]
            pooled_r = [S.res(f"pooled{wi}") for wi in range(4)]
            yT_r = S.res("yT")
            fix_r = S.res("fix")
            pa, pb = pieces
            sa = ensure(pa)
            sb_ = ensure(pb)
            w_in, w_grp, w_out = slot_view(pa, 0), slot_view(pa, 1), slot_view(pb, 0)
            for g in range(NG):
                t = gsl(g)
                require(("norm", n_inst, g))

                def pstage(wi, g=g, t=t):
                    src = pT[:, 2 * wi:2 * wi + 2, :]
                    if g == 0:
                        S.emit("dve", lambda e: [e.memset(src[:, :, 0:16], 0.0)], writes=[pT_r[wi]])
                    else:
                        ACTV(src[:, :, 0:16], src[:, :, 512:528], AF.Copy, [pT_r[wi]], [pT_r[wi]])
                    for pc in (2 * wi, 2 * wi + 1):
                        b = nb([0, 1, 2, 3])
                        PE(banks[b][:], [(w_in[:, kk, pc * 128:(pc + 1) * 128], hT[:, kk, t]) for kk in range(8)],
                           [ring_r[sa], hT_r[g]], [bank_r[b]])
                        ACTV(pT[:, pc, 16:528], banks[b][:], AF.Copy, [bank_r[b]], [pT_r[wi]])

                def poolstage(wi, g=g):
                    w = POOLW[wi]
                    src = pT[:, 2 * wi:2 * wi + 2, :]
                    cur, cur_r = src, pT_r[wi]
                    sh = 1
                    k = 0
                    while sh < w:
                        dst, dst_r = ping[k % 2], ping_r[k % 2]
                        lo = 2 * sh - 1
                        TT("dve", dst[:, :, lo:528], cur[:, :, lo:528], cur[:, :, lo - sh:528 - sh], ALU.add,
                           [cur_r], [dst_r])
                        cur, cur_r = dst, dst_r
                        sh *= 2
                        k += 1
                    STT(pooled[:, 2 * wi:2 * wi + 2, :], cur[:, :, 16:528], 1.0 / w, src[:, :, 16:528],
                        ALU.mult, ALU.subtract, [cur_r, pT_r[wi]], [pooled_r[wi]])
                    if g == 0:
                        TT("dve", fix, cur[:, :, 16:32], invc[:, wi, :].unsqueeze(1).broadcast_to([128, 2, 16]),
                           ALU.mult, [cur_r, const_r], [fix_r])
                        TT("dve", pooled[:, 2 * wi:2 * wi + 2, 0:16], fix, src[:, :, 16:32], ALU.subtract,
                           [fix_r, pT_r[wi], pooled_r[wi]], [pooled_r[wi]])

                def outstage(gp):
                    tp_ = gsl(gp)
                    for oc in range(8):
                        b = nb([0, 1, 2, 3])
                        PE(banks[b][:], [(w_out[:, kk, oc * 128:(oc + 1) * 128], yT[:, kk, :]) for kk in range(8)],
                           [ring_r[sb_], yT_r], [bank_r[b]])
                        TT("dve", xs[:, oc, tp_], banks[b][:], xs[:, oc, tp_], ALU.add,
                           [bank_r[b], xs_r[oc][gp]], [xs_r[oc][gp]])
                        pop_hook()
                    push_post(n_inst, gp)

                pstage(3)
                pop_hook()
                pstage(2)
                poolstage(3)
                pop_hook()
                pstage(1)
                poolstage(2)
                pop_hook()
                pstage(0)
                poolstage(1)
                poolstage(0)
                if g > 0:
                    outstage(g - 1)
                for grp in (3, 2, 1, 0):
                    for o2 in range(2):
                        b = nb([4, 5])
                        PE(banks[b][:], [(w_grp[:, grp * 2 + kc, o2 * 128:(o2 + 1) * 128], pooled[:, grp * 2 + kc, :])
                                         for kc in range(2)],
                           [ring_r[sa], pooled_r[grp]], [bank_r[b]])
                        cc = grp * 2 + o2
                        ACTV(yT[:, cc, :], banks[b][:], AF.Copy, [bank_r[b], const_r], [yT_r],
                             scale=vec[:, VC_BSC + cc:VC_BSC + cc + 1])
                if g == NG - 1:
                    outstage(g)
            done(pa)
            done(pb)
            S.barrier()

        ensure(1)
        for g in range(NG):
            load_group(0, g)
            normq.append((("norm", 0, g), norm_gen(g, gcol_of(seq[0][1]), hT, lambda c, g: hT_r[g])))
        for n_inst, (st_i, sl, pieces) in enumerate(seq):
            if sl[0] == "f":
                emit_ffn(n_inst, int(sl[1]) * 2 + int(sl[2]), pieces)
            elif sl == "mA":
                emit_mixA(n_inst, pieces)
            else:
                emit_mixB(n_inst, pieces)
        while finq or normq or cur_hook[0] is not None:
            pop_hook()

        with nc.Block() as block:
            @block.tensor
            def _(e):
                S.run_engine("pe", e)

            @block.scalar
            def _(e):
                S.run_engine("act", e)

            @block.vector
            def _(e):
                S.run_engine("dve", e)

            @block.gpsimd
            def _(e):
                S.run_engine("pool", e)

            @block.sync
            def _(e):
                S.run_engine("sp", e)
                for g in range(NG):
                    if stx_st[g].count:
                        e.wait_ge(stx_st[g].sem, stx_st[g].count * 16)
    return nc


def host_consts(ffn_norm, mix_norm, final_norm, b_scale, a_v_norm, a_w_s, a_b_s):
    def cols(v):
        return np.ascontiguousarray(np.asarray(v, np.float32).reshape(-1, 128).T)
    vec = np.zeros((128, NVEC), np.float32)
    for i in range(2):
        for k in range(2):
            vec[:, VC_FFN + (i * 2 + k) * 8:VC_FFN + (i * 2 + k) * 8 + 8] = cols(ffn_norm[i, k])
        vec[:, VC_MIX + i * 8:VC_MIX + i * 8 + 8] = cols(mix_norm[i])
    vec[:, VC_FIN:VC_FIN + 8] = cols(final_norm)
    vec[:, VC_BSC:VC_BSC + 8] = cols(b_scale[0])
    vec[:, VC_VN:VC_VN + 16] = cols(a_v_norm[0])
    wsT = np.ascontiguousarray(np.transpose(np.asarray(a_w_s[0], np.float32), (2, 0, 1)).reshape(128, 1024))
    s_idx = np.arange(128)[:, None]
    t_idx = np.arange(128)[None, :]
    mask = (s_idx <= t_idx).astype(np.float32)
    bias_bc = np.ascontiguousarray(np.broadcast_to(np.asarray(a_b_s[0], np.float32).reshape(1, 1024), (128, 1024)))
    invc = np.zeros((128, 4, 16), np.float32)
    for wi, w in enumerate(POOLW):
        invc[:, wi, :] = 1.0 / np.minimum(np.arange(1, 17), w).astype(np.float32)
    return vec, wsT, mask, bias_bc, invc.reshape(128, 64)


def kernel(x, ffn_norm, ffn_w_in, ffn_w_out, mix_norm, a_w_in, a_v_norm, a_w_s, a_b_s,
           a_w_out, b_w_in, b_w_grp, b_scale, b_w_out, final_norm):
    x = np.asarray(x, np.float32)
    B, Sq, Dm = x.shape
    vec, wsT, mask, bias_bc, invc = host_consts(ffn_norm, mix_norm, final_norm, b_scale, a_v_norm, a_w_s, a_b_s)
    shared = {
        "ffn_w_in": np.ascontiguousarray(np.asarray(ffn_w_in, np.float32).reshape(4, D, 2 * DFF)),
        "ffn_w_out": np.ascontiguousarray(np.asarray(ffn_w_out, np.float32).reshape(4, DFF, D)),
        "a_w_in": np.ascontiguousarray(np.asarray(a_w_in, np.float32)[0]),
        "a_w_out": np.ascontiguousarray(np.asarray(a_w_out, np.float32)[0]),
        "b_w_in": np.ascontiguousarray(np.asarray(b_w_in, np.float32)[0]),
        "b_w_grp": np.ascontiguousarray(np.asarray(b_w_grp, np.float32)[0].reshape(1024, 256)),
        "b_w_out": np.ascontiguousarray(np.asarray(b_w_out, np.float32)[0]),
        "vec": vec, "wsT": wsT, "mask": mask, "bias_bc": bias_bc, "invc": invc,
    }
    per = B // NCORE
    xr = x.reshape(NCORE, per * Sq, Dm)
    in_maps = []
    for c in range(NCORE):
        m = dict(shared)
        m["xT"] = np.ascontiguousarray(xr[c].T)
        in_maps.append(m)
    nc = build_program(n_st=per)
    res = run_bass_kernel_spmd(nc, in_maps, core_ids=list(range(NCORE)))
    out = np.empty((NCORE, per * Sq, Dm), np.float32)
    for c in range(NCORE):
        out[c] = np.asarray(res.results[c]["outT"], np.float32).T
    return out.reshape(B, Sq, Dm)
```

```python
import collections
import contextlib
import numpy as np
import concourse.bass as bass
import concourse.mybir as mybir
from concourse.bass_utils import run_bass_kernel_spmd

F32 = mybir.dt.float32
BF16 = mybir.dt.bfloat16
AF = mybir.ActivationFunctionType
ALU = mybir.AluOpType
AX = mybir.AxisListType

NCORE = 8
D = 1024
DFF = 2816
STK = 2048
GK = 512
NG = 4
EPS = 1e-6
POOLW = (2, 4, 8, 16)
FFN_PARTS = [(0, 4), (4, 4), (8, 3), (11, 3), (14, 4), (18, 4)]
SLOT = 12288
ARENA_F32 = 12288

VC_FFN = 0
VC_MIX = 32
VC_FIN = 48
VC_BSC = 56
VC_VN = 64
NVEC = 80


class Stream:
    def __init__(self, name, step):
        self.name = name
        self.step = step
        self.count = 0
        self.sem = None


class Res:
    def __init__(self, name, readers=None):
        self.name = name
        self.writers = {}
        self.readers = dict(readers) if readers else {}


class Sched:
    ENGS = ("pe", "act", "dve", "pool", "sp")

    def __init__(self):
        self.ops = {e: [] for e in self.ENGS}
        self.waited = {e: {} for e in self.ENGS}
        self.streams = []
        self.cstream = {e: self.new_stream(e, 1) for e in ("pe", "act", "dve", "pool")}
        self.phase_res = []
        self.prev_users = {}

    def new_stream(self, name, step):
        s = Stream(name, step)
        self.streams.append(s)
        return s

    def res(self, name):
        r = Res(name, self.prev_users)
        self.phase_res.append(r)
        return r

    def barrier(self):
        pu = dict(self.prev_users)
        for r in self.phase_res:
            for d in (r.readers, r.writers):
                for s, c in d.items():
                    if pu.get(s, 0) < c:
                        pu[s] = c
        self.prev_users = pu
        self.phase_res = []

    def emit(self, eng, fn, reads=(), writes=(), stream=None, nsig=1):
        own = self.cstream.get(eng)
        need = {}

        def add(d, skip_own):
            for s, c in d.items():
                if skip_own and s is own:
                    continue
                if need.get(s, 0) < c:
                    need[s] = c

        for r in reads:
            add(r.writers, False)
        for w in writes:
            add(w.readers, False)
            add(w.writers, False)
        wd = self.waited[eng]
        waits = []
        for s, c in need.items():
            if wd.get(s, 0) < c:
                wd[s] = c
                waits.append((s, c))
        st = stream if stream is not None else own
        st.count += nsig
        tok = st.count
        self.ops[eng].append((waits, fn, st, nsig))
        for r in reads:
            if r.readers.get(st, 0) < tok:
                r.readers[st] = tok
        for w in writes:
            w.writers = {st: tok}
            w.readers = {}

    def run_engine(self, eng, e):
        for waits, fn, st, nsig in self.ops[eng]:
            for s, c in waits:
                e.wait_ge(s.sem, c * s.step)
            inss = fn(e)
            assert len(inss) == nsig
            for ins in inss:
                ins.then_inc(st.sem, st.step)


def mm_group(e, out, pairs):
    n = len(pairs)
    ins = None
    for i, (l, r) in enumerate(pairs):
        ins = e.matmul(out, l, r, start=(i == 0), stop=(i == n - 1))
    return [ins]


def build_program(n_st=2, sublayers=("f00", "mA", "f01", "f10", "mB", "f11"), final=True):
    nc = bass.Bass("TRN2", target_bir_lowering=False)
    ntok = n_st * STK

    def din(name, shape):
        return nc.dram_tensor(name, list(shape), F32, kind="ExternalInput").ap()

    xT = din("xT", [D, ntok])
    outT = nc.dram_tensor("outT", [D, ntok], F32, kind="ExternalOutput").ap()
    ffn_w_in = din("ffn_w_in", [4, D, 2 * DFF])
    ffn_w_out = din("ffn_w_out", [4, DFF, D])
    a_w_in = din("a_w_in", [D, 4096])
    a_w_out = din("a_w_out", [2048, D])
    b_w_in = din("b_w_in", [D, D])
    b_w_grp = din("b_w_grp", [1024, 256])
    b_w_out = din("b_w_out", [D, D])
    vec_d = din("vec", [128, NVEC])
    wsT_d = din("wsT", [128, 1024])
    mask_d = din("mask", [128, 128])
    bias_d = din("bias_bc", [128, 1024])
    invc_d = din("invc", [128, 64])

    xT3 = xT.rearrange("(c p) t -> p c t", p=128)
    outT3 = outT.rearrange("(c p) t -> p c t", p=128)

    S = Sched()
    es = contextlib.ExitStack()
    with es:
        def sb(name, shape, dt):
            return es.enter_context(nc.sbuf_tensor(name, list(shape), dt))

        xs = sb("xs", [128, 8, STK], F32)
        hT = sb("hT", [128, 8, STK], BF16)
        ring = sb("ring", [128, 2, SLOT], BF16)
        arena = sb("arena", [128, ARENA_F32], F32)
        vec = sb("vec_t", [128, NVEC], F32)
        ones = sb("ones", [128, 128], BF16)
        wsT = sb("wsT_t", [128, 8, 128], BF16)
        bias_bc = sb("bias_t", [128, 8, 128], F32)
        invc = sb("invc_t", [128, 4, 16], F32)
        epst = sb("eps_t", [128, 1], F32)
        ss = sb("ss_t", [128, 32], F32)
        ssum = sb("ssum_t", [128, 8], F32)
        rt8 = sb("rt8_t", [128, 8], F32)
        rstdv = sb("rstdv_t", [128, 8], F32)
        sq = sb("sq_t", [128, 4, 512], BF16)
        rstd_bufs = [sb("rstd_t", [128, 512], F32), sb("rstd2_t", [128, 512], F32)]
        banks = [es.enter_context(nc.psum_tensor(f"bank{i}", [128, 512], F32)) for i in range(8)]

        for s in S.streams:
            s.sem = es.enter_context(nc.semaphore(f"sem_{s.name}"))

        def dstream(name):
            s = S.new_stream(name, 16)
            s.sem = es.enter_context(nc.semaphore(f"sem_{name}"))
            return s

        ring_st = [dstream("ring0"), dstream("ring1")]
        cst_st = dstream("cst")
        ldx_st = [dstream(f"ldx{g}") for g in range(NG)]
        ldxb_st = [dstream(f"ldxb{g}") for g in range(NG)]
        stx_st = [dstream(f"stx{g}") for g in range(NG)]

        xs_r = [[Res(f"xs{c}_{g}") for g in range(NG)] for c in range(8)]
        hT_r = [Res(f"hT{g}") for g in range(NG)]
        ring_r = [Res("ring0"), Res("ring1")]
        bank_r = [Res(f"bank{i}") for i in range(8)]
        const_r = Res("const")
        ones_r = Res("ones")
        wsT_r = Res("wsT")
        small_r = Res("small")
        sq_r = Res("sq")
        rstd_rs = [Res("rstd"), Res("rstd2")]
        ncount = [0]

        def carve(off_bytes, shape, dt):
            n = int(np.prod(shape))
            o = off_bytes // 4
            if dt == F32:
                a = arena[:, o:o + n]
            else:
                a = arena[:, o:o + n // 2].bitcast(BF16)
            if len(shape) == 2:
                a = a.rearrange("p (a b) -> p a b", a=shape[0])
            return a

        KB = 1024

        def dma_op(eng, pairs, stream, reads=(), writes=()):
            def fn(e, pairs=pairs):
                return [e.dma_start(out=o, in_=i) for (o, i) in pairs]
            S.emit(eng, fn, reads=reads, writes=writes, stream=stream, nsig=len(pairs))

        wsraw = carve(0, (8, 128), F32)
        maskt = carve(4 * KB, (128,), F32)
        setup_r = S.res("setup")
        dma_op("sp", [(vec[:], vec_d[:, :]),
                      (bias_bc[:].rearrange("p h t -> p (h t)"), bias_d[:, :]),
                      (invc[:].rearrange("p a b -> p (a b)"), invc_d[:, :]),
                      (wsraw.rearrange("p h t -> p (h t)"), wsT_d[:, :]),
                      (maskt, mask_d[:, :])],
               cst_st, writes=[const_r, setup_r])
        S.emit("dve", lambda e: [e.memset(ones[:], 1.0)], writes=[ones_r])
        S.emit("dve", lambda e: [e.memset(epst[:], EPS)], writes=[const_r])
        S.emit("dve", lambda e: [e.tensor_tensor(
            out=wsT[:], in0=wsraw, in1=maskt.unsqueeze(1).broadcast_to([128, 8, 128]), op=ALU.mult)],
            reads=[setup_r, const_r], writes=[wsT_r])
        S.barrier()

        def wv(ap2d, c0, c1):
            return ap2d.rearrange("(k p) c -> p k c", p=128)[:, :, c0:c1]

        def rows(ap2d, r0, nchunk):
            return ap2d[r0 * 128:(r0 + nchunk) * 128, :].rearrange("(j p) c -> p j c", p=128)

        plan = []

        def plan_ffn(widx):
            idx = []
            for (j0, n) in FFN_PARTS:
                w_in = ffn_w_in[widx]
                w_out = ffn_w_out[widx]
                plan.append([(0, 8, n * 128, wv(w_in, j0 * 128, (j0 + n) * 128)),
                             (4096, 8, n * 128, wv(w_in, DFF + j0 * 128, DFF + (j0 + n) * 128)),
                             (8192, n, 1024, rows(w_out, j0, n))])
                idx.append(len(plan) - 1)
            return idx

        def plan_mixA():
            idx = []
            for blk in range(2):
                for vp in range(2):
                    plan.append([(0, 8, 1024, wv(a_w_in, 2048 + vp * 1024, 2048 + (vp + 1) * 1024))])
                    idx.append(len(plan) - 1)
                for hp in range(4):
                    plan.append([(0, 8, 512, wv(a_w_in, hp * 512, (hp + 1) * 512)),
                                 (4096, 4, 1024, rows(a_w_out, hp * 4, 4))])
                    idx.append(len(plan) - 1)
            return idx

        def plan_mixB():
            plan.append([(0, 8, 1024, wv(b_w_in, 0, 1024)),
                         (8192, 8, 256, rows(b_w_grp, 0, 8))])
            plan.append([(0, 8, 1024, wv(b_w_out, 0, 1024))])
            return [len(plan) - 2, len(plan) - 1]

        seq = []
        for st_i in range(n_st):
            for sl in sublayers:
                if sl[0] == "f":
                    i, k = int(sl[1]), int(sl[2])
                    seq.append((st_i, sl, plan_ffn(i * 2 + k)))
                elif sl == "mA":
                    seq.append((st_i, sl, plan_mixA()))
                else:
                    seq.append((st_i, sl, plan_mixB()))

        loaded = [0]

        def slot_view(p, j):
            off, k, c, _ = plan[p][j]
            return ring[:, p % 2, off:off + k * c].rearrange("p (k c) -> p k c", k=k)

        def ensure(p):
            while loaded[0] <= min(p, len(plan) - 1):
                q = loaded[0]
                pairs = [(slot_view(q, j), plan[q][j][3]) for j in range(len(plan[q]))]
                dma_op("pool", pairs, ring_st[q % 2], writes=[ring_r[q % 2]])
                loaded[0] += 1
            return p % 2

        def done(p):
            ensure(p + 2)

        bank_rot = [0]

        def nb(lst):
            b = lst[bank_rot[0] % len(lst)]
            bank_rot[0] += 1
            return b

        def gsl(g):
            return slice(g * GK, (g + 1) * GK)

        def PE(out, pairs, reads, writes):
            S.emit("pe", lambda e: mm_group(e, out, pairs), reads=reads, writes=writes)

        def ACTV(out, in_, func, reads, writes, **kw):
            S.emit("act", lambda e: [e.activation(out=out, in_=in_, func=func, **kw)], reads=reads, writes=writes)

        def TT(eng, out, in0, in1, op, reads, writes):
            S.emit(eng, lambda e: [e.tensor_tensor(out=out, in0=in0, in1=in1, op=op)], reads=reads, writes=writes)

        def STT(out, in0, scalar, in1, op0, op1, reads, writes):
            S.emit("dve", lambda e: [e.scalar_tensor_tensor(out=out, in0=in0, scalar=scalar, in1=in1,
                                                            op0=op0, op1=op1)], reads=reads, writes=writes)

        def norm_gen(g, gcol, dst, dst_r):
            t = gsl(g)
            rstd = rstd_bufs[ncount[0] % 2]
            rstd_r = rstd_rs[ncount[0] % 2]
            ncount[0] += 1
            for half in range(2):
                def sqfn(e, half=half):
                    ins = None
                    for cc in range(4):
                        ins = e.activation(out=sq[:, cc, :], in_=xs[:, half * 4 + cc, t], func=AF.Square)
                    return [ins]
                S.emit("act", sqfn, reads=[xs_r[half * 4 + cc][g] for cc in range(4)], writes=[sq_r])
                yield

                def stfn(e, half=half):
                    ins = None
                    for cc in range(4):
                        ins = e.matmul(banks[6][:], ones[:], sq[:, cc, :],
                                       start=(half == 0 and cc == 0), stop=(half == 1 and cc == 3))
                    return [ins]
                S.emit("pe", stfn, reads=[sq_r, ones_r], writes=[bank_r[6]])
                yield ("heavy" if half == 1 else None)
            ACTV(rstd[:], banks[6][:], AF.Ln, [bank_r[6], const_r], [rstd_r], bias=epst[:, 0:1], scale=1.0 / D)
            ACTV(rstd[:], rstd[:], AF.Exp, [rstd_r], [rstd_r], scale=-0.5)
            yield
            for c in range(8):
                if c == 4:
                    yield
                STT(dst[:, c, t], xs[:, c, t], vec[:, gcol + c:gcol + c + 1], rstd[:], ALU.mult, ALU.mult,
                    [xs_r[c][g], rstd_r, const_r], [dst_r(c, g)])

        finq = collections.deque()
        normq = collections.deque()
        hook_done = set()
        cur_hook = [None]

        def pop_hook(kind="any"):
            if cur_hook[0] is None:
                q = finq if finq else normq
                if not q:
                    return
                key, gen = q.popleft()
                cur_hook[0] = [key, gen, None]
            key, gen, tag = cur_hook[0]
            if tag == "heavy" and kind == "light":
                return
            if kind == "heavy_only" and tag != "heavy":
                return
            try:
                cur_hook[0][2] = next(gen)
            except StopIteration:
                hook_done.add(key)
                cur_hook[0] = None

        def flush_fins():
            while finq or (cur_hook[0] is not None and cur_hook[0][0][0] == "fin"):
                pop_hook()

        def require(key):
            while key not in hook_done:
                assert cur_hook[0] is not None or finq or normq, f"missing hook {key}"
                pop_hook()

        def gcol_of(sl):
            if sl[0] == "f":
                return VC_FFN + (int(sl[1]) * 2 + int(sl[2])) * 8
            return VC_MIX + (0 if sl == "mA" else 8)

        def load_group(st_i, g):
            tsl = slice(st_i * STK + g * GK, st_i * STK + (g + 1) * GK)
            dma_op("sp", [(xs[:, 0:4, gsl(g)], xT3[:, 0:4, tsl])], ldx_st[g],
                   writes=[xs_r[c][g] for c in range(4)])
            dma_op("sp", [(xs[:, 4:8, gsl(g)], xT3[:, 4:8, tsl])], ldxb_st[g],
                   writes=[xs_r[c][g] for c in range(4, 8)])

        def push_post(n, g):
            st_i = seq[n][0]
            last_of_st = (n + 1 == len(seq)) or (seq[n + 1][0] != st_i)
            if last_of_st:
                def fin(st_i=st_i, g=g):
                    if final:
                        yield from norm_gen(g, VC_FIN, xs, lambda c, g: xs_r[c][g])
                    tsl = slice(st_i * STK + g * GK, st_i * STK + (g + 1) * GK)
                    dma_op("sp", [(outT3[:, :, tsl], xs[:, :, gsl(g)])], stx_st[g],
                           reads=[xs_r[c][g] for c in range(8)])
                    if n + 1 < len(seq):
                        load_group(seq[n + 1][0], g)
                finq.append((("fin", n, g), fin()))
            if n + 1 < len(seq):
                normq.append((("norm", n + 1, g), norm_gen(g, gcol_of(seq[n + 1][1]), hT, lambda c, g: hT_r[g])))

        def emit_ffn(n_inst, widx, pieces):
            actT = [carve(0, (4, 512), BF16), carve(4 * KB, (4, 512), BF16)]
            sg = [carve(8 * KB, (512,), F32), carve(10 * KB, (512,), F32)]
            actT_r = [S.res("actT0"), S.res("actT1")]
            sg_r = [S.res("sg0"), S.res("sg1")]
            gcol = VC_FFN + widx * 8
            pairs = [(q, g) for q in range(len(FFN_PARTS)) for g in range(NG)]
            cnt = [0]
            ocnt = [0]
            nxt_mixer = (n_inst + 1 < len(seq) and seq[n_inst + 1][0] == seq[n_inst][0]
                         and seq[n_inst + 1][1][0] == "m")

            def gu_kind(q):
                if q == 0:
                    return "any"
                return "light"

            def GU(i):
                q, g = pairs[i]
                j0, n = FFN_PARTS[q]
                if g == 0:
                    ensure(pieces[q])
                if q == 0:
                    require(("norm", n_inst, g))
                p = pieces[q]
                s = p % 2
                wg, wu = slot_view(p, 0), slot_view(p, 1)
                t = gsl(g)
                a = i % 2
                for jj in range(n):
                    m = cnt[0]
                    cnt[0] += 1
                    bG, bU = m % 2, 2 + m % 2
                    PE(banks[bG][:], [(wg[:, kk, jj * 128:(jj + 1) * 128], hT[:, kk, t]) for kk in range(8)],
                       [ring_r[s], hT_r[g]], [bank_r[bG]])
                    pop_hook(gu_kind(q))
                    PE(banks[bU][:], [(wu[:, kk, jj * 128:(jj + 1) * 128], hT[:, kk, t]) for kk in range(8)],
                       [ring_r[s], hT_r[g]], [bank_r[bU]])
                    ACTV(sg[m % 2], banks[bG][:], AF.Silu, [bank_r[bG]], [sg_r[m % 2]])
                    TT("dve", actT[a][:, jj, :], sg[m % 2], banks[bU][:], ALU.mult,
                       [sg_r[m % 2], bank_r[bU]], [actT_r[a]])
                    pop_hook(gu_kind(q))
                    if q == 0 or finq:
                        pop_hook(gu_kind(q))

            def WO(i):
                q, g = pairs[i]
                j0, n = FFN_PARTS[q]
                p = pieces[q]
                s = p % 2
                wo = slot_view(p, 2)
                t = gsl(g)
                a = i % 2
                pop_hook("heavy_only")
                for dc in range(8):
                    bO = (4, 5, 7)[ocnt[0] % 3]
                    ocnt[0] += 1
                    PE(banks[bO][:], [(wo[:, jj, dc * 128:(dc + 1) * 128], actT[a][:, jj, :]) for jj in range(n)],
                       [ring_r[s], actT_r[a]], [bank_r[bO]])
                    STT(xs[:, dc, t], banks[bO][:], 0.5, xs[:, dc, t], ALU.mult, ALU.add,
                        [bank_r[bO], xs_r[dc][g]], [xs_r[dc][g]])
                if g == NG - 1:
                    done(p)
                if q == len(FFN_PARTS) - 1:
                    push_post(n_inst, g)

            GU(0)
            for i in range(1, len(pairs)):
                GU(i)
                WO(i - 1)
            WO(len(pairs) - 1)
            flush_fins()
            S.barrier()

        def emit_mixA(n_inst, pieces):
            for g in range(NG):
                require(("norm", n_inst, g))
            v = carve(0, (8, 2048), BF16)
            uT = carve(32 * KB, (2, 512), F32)
            ugT = [carve(36 * KB, (4, 512), BF16), carve(40 * KB, (4, 512), BF16)]
            tmp = [carve(44 * KB, (512,), F32), carve(46 * KB, (512,), F32)]
            junk = ugT[0][:, 0, :]
            pi = 0
            ucnt = 0
            for blk in range(2):
                require(("norm", n_inst, blk * 2))
                require(("norm", n_inst, blk * 2 + 1))
                v_r = [S.res(f"v{tt}") for tt in range(8)]
                uT_r = [S.res("uT0"), S.res("uT1")]
                ugT_r = [S.res("ugT0"), S.res("ugT1")]
                tmp_r = [S.res("tmp0"), S.res("tmp1")]
                S.emit("dve", lambda e: [e.memset(ss[:], 0.0)], reads=[small_r], writes=[small_r])
                pend = None
                for vp in range(2):
                    p = pieces[pi]
                    pi += 1
                    s = ensure(p)
                    wvv = slot_view(p, 0)
                    for vb in range(2):
                        col = (vp * 2 + vb) * 512
                        for tt in range(8):
                            pop_hook()
                            g = blk * 2 + tt // 4
                            tok = slice((blk * 8 + tt) * 128, (blk * 8 + tt + 1) * 128)
                            b = nb([0, 1, 2, 3])
                            PE(banks[b][:], [(hT[:, kk, tok], wvv[:, kk, vb * 512:(vb + 1) * 512]) for kk in range(8)],
                               [ring_r[s], hT_r[g]], [bank_r[b]])
                            ACTV(v[:, tt, col:col + 512], banks[b][:], AF.Gelu_apprx_tanh, [bank_r[b]], [v_r[tt]])
                            if pend is not None:
                                ACTV(*pend[0], **pend[1])
                            sidx = tt * 4 + vp * 2 + vb
                            pend = ((junk, v[:, tt, col:col + 512], AF.Square, [v_r[tt]], [ugT_r[0], small_r]),
                                    dict(accum_out=ss[:, sidx:sidx + 1]))
                    done(p)
                ACTV(*pend[0], **pend[1])
                S.emit("dve", lambda e: [e.tensor_reduce(
                    out=ssum[:], in_=ss[:].rearrange("p (t f) -> p t f", f=4), axis=AX.X, op=ALU.add)],
                    reads=[small_r], writes=[small_r])
                ACTV(rt8[:], ssum[:], AF.Sqrt, [small_r, const_r], [small_r], bias=epst[:, 0:1], scale=1.0 / 2048)
                S.emit("dve", lambda e: [e.reciprocal(out=rstdv[:], in_=rt8[:])], reads=[small_r], writes=[small_r])
                for tt in range(8):
                    def vn(e, tt=tt):
                        return [e.tensor_scalar_mul(out=v[:, tt, :], in0=v[:, tt, :], scalar1=rstdv[:, tt:tt + 1])]
                    S.emit("dve", vn, reads=[v_r[tt], small_r], writes=[v_r[tt]])
                pendWO = [None]

                def run_pending():
                    if pendWO[0] is not None:
                        pendWO[0]()
                        pendWO[0] = None

                for hp in range(4):
                    p = pieces[pi]
                    pi += 1
                    s = ensure(p)
                    wu, wo = slot_view(p, 0), slot_view(p, 1)
                    for gi in range(2):
                        g = blk * 2 + gi
                        t = gsl(g)
                        a = ucnt % 2
                        ucnt += 1
                        for hh in range(2):
                            h = hp * 2 + hh
                            for uc in range(2):
                                b = nb([0, 1])
                                cu = (hh * 2 + uc) * 128
                                PE(banks[b][:], [(wu[:, kk, cu:cu + 128], hT[:, kk, t]) for kk in range(8)],
                                   [ring_r[s], hT_r[g]], [bank_r[b]])
                                ACTV(uT[:, uc, :], banks[b][:], AF.Gelu_apprx_tanh, [bank_r[b]], [uT_r[uc]])
                                pop_hook()
                            for dc in range(2):
                                b = nb([2, 3])
                                gp = [(banks[b][:, tl * 128:(tl + 1) * 128],
                                       v[:, gi * 4 + tl, h * 256 + dc * 128:h * 256 + (dc + 1) * 128],
                                       wsT[:, h, :]) for tl in range(4)]

                                def gate_mm(e, gp=gp):
                                    ins = None
                                    for (o, l, r) in gp:
                                        ins = e.matmul(o, l, r, start=True, stop=True)
                                    return [ins]
                                S.emit("pe", gate_mm, reads=[v_r[gi * 4 + tl] for tl in range(4)] + [wsT_r],
                                       writes=[bank_r[b]])
                                vc = VC_VN + h * 2 + dc
                                STT(tmp[dc].rearrange("p (a t) -> p a t", a=4),
                                    banks[b][:].rearrange("p (a t) -> p a t", a=4),
                                    vec[:, vc:vc + 1],
                                    bias_bc[:, h, :].unsqueeze(1).broadcast_to([128, 4, 128]),
                                    ALU.mult, ALU.add, [bank_r[b], const_r], [tmp_r[dc]])
                                TT("pool", ugT[a][:, hh * 2 + dc, :], tmp[dc], uT[:, dc, :], ALU.mult,
                                   [tmp_r[dc], uT_r[dc]], [ugT_r[a]])
                                pop_hook()
                        run_pending()

                        def wo_fn(p=p, s=s, wo=wo, g=g, t=t, a=a, hp=hp, gi=gi):
                            for oc in range(8):
                                b = nb([4, 5, 7])
                                PE(banks[b][:], [(wo[:, cc, oc * 128:(oc + 1) * 128], ugT[a][:, cc, :]) for cc in range(4)],
                                   [ring_r[s], ugT_r[a]], [bank_r[b]])
                                TT("dve", xs[:, oc, t], banks[b][:], xs[:, oc, t], ALU.add,
                                   [bank_r[b], xs_r[oc][g]], [xs_r[oc][g]])
                            if hp == 3:
                                push_post(n_inst, g)
                            if gi == 1:
                                done(p)
                        pendWO[0] = wo_fn
                run_pending()
                S.barrier()

        def emit_mixB(n_inst, pieces):
            for g in range(NG):
                require(("norm", n_inst, g))
            pT = carve(0, (8, 528), F32)
            ping = [carve(16896, (2, 528), F32), carve(16896 + 4224, (2, 528), F32)]
            pooled = carve(25344, (8, 512), BF16)
            yT = carve(33536, (8, 512), BF16)
            fix = carve(41728, (2, 16), F32)
            pT_r = [S.res(f"pT{wi}") for wi in range(4)]
            ping_r = [S.res("ping0"), S.res("ping1")]
            pooled_r = [S.res(f"pooled{wi}") for wi in range(4)]
            yT_r = S.res("yT")
            fix_r = S.res("fix")
            pa, pb = pieces
            sa = ensure(pa)
            sb_ = ensure(pb)
            w_in, w_grp, w_out = slot_view(pa, 0), slot_view(pa, 1), slot_view(pb, 0)
            for g in range(NG):
                t = gsl(g)
                require(("norm", n_inst, g))

                def pstage(wi, g=g, t=t):
                    src = pT[:, 2 * wi:2 * wi + 2, :]
                    if g == 0:
                        S.emit("dve", lambda e: [e.memset(src[:, :, 0:16], 0.0)], writes=[pT_r[wi]])
                    else:
                        ACTV(src[:, :, 0:16], src[:, :, 512:528], AF.Copy, [pT_r[wi]], [pT_r[wi]])
                    for pc in (2 * wi, 2 * wi + 1):
                        b = nb([0, 1, 2, 3])
                        PE(banks[b][:], [(w_in[:, kk, pc * 128:(pc + 1) * 128], hT[:, kk, t]) for kk in range(8)],
                           [ring_r[sa], hT_r[g]], [bank_r[b]])
                        ACTV(pT[:, pc, 16:528], banks[b][:], AF.Copy, [bank_r[b]], [pT_r[wi]])

                def poolstage(wi, g=g):
                    w = POOLW[wi]
                    src = pT[:, 2 * wi:2 * wi + 2, :]
                    cur, cur_r = src, pT_r[wi]
                    sh = 1
                    k = 0
                    while sh < w:
                        dst, dst_r = ping[k % 2], ping_r[k % 2]
                        lo = 2 * sh - 1
                        TT("dve", dst[:, :, lo:528], cur[:, :, lo:528], cur[:, :, lo - sh:528 - sh], ALU.add,
                           [cur_r], [dst_r])
                        cur, cur_r = dst, dst_r
                        sh *= 2
                        k += 1
                    STT(pooled[:, 2 * wi:2 * wi + 2, :], cur[:, :, 16:528], 1.0 / w, src[:, :, 16:528],
                        ALU.mult, ALU.subtract, [cur_r, pT_r[wi]], [pooled_r[wi]])
                    if g == 0:
                        TT("dve", fix, cur[:, :, 16:32], invc[:, wi, :].unsqueeze(1).broadcast_to([128, 2, 16]),
                           ALU.mult, [cur_r, const_r], [fix_r])
                        TT("dve", pooled[:, 2 * wi:2 * wi + 2, 0:16], fix, src[:, :, 16:32], ALU.subtract,
                           [fix_r, pT_r[wi], pooled_r[wi]], [pooled_r[wi]])

                def outstage(gp):
                    tp_ = gsl(gp)
                    for oc in range(8):
                        b = nb([0, 1, 2, 3])
                        PE(banks[b][:], [(w_out[:, kk, oc * 128:(oc + 1) * 128], yT[:, kk, :]) for kk in range(8)],
                           [ring_r[sb_], yT_r], [bank_r[b]])
                        TT("dve", xs[:, oc, tp_], banks[b][:], xs[:, oc, tp_], ALU.add,
                           [bank_r[b], xs_r[oc][gp]], [xs_r[oc][gp]])
                        pop_hook()
                    push_post(n_inst, gp)

                pstage(3)
                pop_hook()
                pstage(2)
                poolstage(3)
                pop_hook()
                pstage(1)
                poolstage(2)
                pop_hook()
                pstage(0)
                poolstage(1)
                poolstage(0)
                if g > 0:
                    outstage(g - 1)
                for grp in (3, 2, 1, 0):
                    for o2 in range(2):
                        b = nb([4, 5])
                        PE(banks[b][:], [(w_grp[:, grp * 2 + kc, o2 * 128:(o2 + 1) * 128], pooled[:, grp * 2 + kc, :])
                                         for kc in range(2)],
                           [ring_r[sa], pooled_r[grp]], [bank_r[b]])
                        cc = grp * 2 + o2
                        ACTV(yT[:, cc, :], banks[b][:], AF.Copy, [bank_r[b], const_r], [yT_r],
                             scale=vec[:, VC_BSC + cc:VC_BSC + cc + 1])
                if g == NG - 1:
                    outstage(g)
            done(pa)
            done(pb)
            S.barrier()

        ensure(1)
        for g in range(NG):
            load_group(0, g)
            normq.append((("norm", 0, g), norm_gen(g, gcol_of(seq[0][1]), hT, lambda c, g: hT_r[g])))
        for n_inst, (st_i, sl, pieces) in enumerate(seq):
            if sl[0] == "f":
                emit_ffn(n_inst, int(sl[1]) * 2 + int(sl[2]), pieces)
            elif sl == "mA":
                emit_mixA(n_inst, pieces)
            else:
                emit_mixB(n_inst, pieces)
        while finq or normq or cur_hook[0] is not None:
            pop_hook()

        with nc.Block() as block:
            @block.tensor
            def _(e):
                S.run_engine("pe", e)

            @block.scalar
            def _(e):
                S.run_engine("act", e)

            @block.vector
            def _(e):
                S.run_engine("dve", e)

            @block.gpsimd
            def _(e):
                S.run_engine("pool", e)

            @block.sync
            def _(e):
                S.run_engine("sp", e)
                for g in range(NG):
                    if stx_st[g].count:
                        e.wait_ge(stx_st[g].sem, stx_st[g].count * 16)
    return nc


def host_consts(ffn_norm, mix_norm, final_norm, b_scale, a_v_norm, a_w_s, a_b_s):
    def cols(v):
        return np.ascontiguousarray(np.asarray(v, np.float32).reshape(-1, 128).T)
    vec = np.zeros((128, NVEC), np.float32)
    for i in range(2):
        for k in range(2):
            vec[:, VC_FFN + (i * 2 + k) * 8:VC_FFN + (i * 2 + k) * 8 + 8] = cols(ffn_norm[i, k])
        vec[:, VC_MIX + i * 8:VC_MIX + i * 8 + 8] = cols(mix_norm[i])
    vec[:, VC_FIN:VC_FIN + 8] = cols(final_norm)
    vec[:, VC_BSC:VC_BSC + 8] = cols(b_scale[0])
    vec[:, VC_VN:VC_VN + 16] = cols(a_v_norm[0])
    wsT = np.ascontiguousarray(np.transpose(np.asarray(a_w_s[0], np.float32), (2, 0, 1)).reshape(128, 1024))
    s_idx = np.arange(128)[:, None]
    t_idx = np.arange(128)[None, :]
    mask = (s_idx <= t_idx).astype(np.float32)
    bias_bc = np.ascontiguousarray(np.broadcast_to(np.asarray(a_b_s[0], np.float32).reshape(1, 1024), (128, 1024)))
    invc = np.zeros((128, 4, 16), np.float32)
    for wi, w in enumerate(POOLW):
        invc[:, wi, :] = 1.0 / np.minimum(np.arange(1, 17), w).astype(np.float32)
    return vec, wsT, mask, bias_bc, invc.reshape(128, 64)


def kernel(x, ffn_norm, ffn_w_in, ffn_w_out, mix_norm, a_w_in, a_v_norm, a_w_s, a_b_s,
           a_w_out, b_w_in, b_w_grp, b_scale, b_w_out, final_norm):
    x = np.asarray(x, np.float32)
    B, Sq, Dm = x.shape
    vec, wsT, mask, bias_bc, invc = host_consts(ffn_norm, mix_norm, final_norm, b_scale, a_v_norm, a_w_s, a_b_s)
    shared = {
        "ffn_w_in": np.ascontiguousarray(np.asarray(ffn_w_in, np.float32).reshape(4, D, 2 * DFF)),
        "ffn_w_out": np.ascontiguousarray(np.asarray(ffn_w_out, np.float32).reshape(4, DFF, D)),
        "a_w_in": np.ascontiguousarray(np.asarray(a_w_in, np.float32)[0]),
        "a_w_out": np.ascontiguousarray(np.asarray(a_w_out, np.float32)[0]),
        "b_w_in": np.ascontiguousarray(np.asarray(b_w_in, np.float32)[0]),
        "b_w_grp": np.ascontiguousarray(np.asarray(b_w_grp, np.float32)[0].reshape(1024, 256)),
        "b_w_out": np.ascontiguousarray(np.asarray(b_w_out, np.float32)[0]),
        "vec": vec, "wsT": wsT, "mask": mask, "bias_bc": bias_bc, "invc": invc,
    }
    per = B // NCORE
    xr = x.reshape(NCORE, per * Sq, Dm)
    in_maps = []
    for c in range(NCORE):
        m = dict(shared)
        m["xT"] = np.ascontiguousarray(xr[c].T)
        in_maps.append(m)
    nc = build_program(n_st=per)
    res = run_bass_kernel_spmd(nc, in_maps, core_ids=list(range(NCORE)))
    out = np.empty((NCORE, per * Sq, Dm), np.float32)
    for c in range(NCORE):
        out[c] = np.asarray(res.results[c]["outT"], np.float32).T
    return out.reshape(B, Sq, Dm)
```
